# Optimizing a Trainium2 kernel written in Bass

```python
import math
import jax, jax.numpy as jnp
from jax import lax
import numpy as np

D_MODEL = 1024
BATCH = 4
SEQ = 8192
DEPTH = 2

CHUNK = 64
HEAD_DIM = 64
A_HEADS = (D_MODEL // 2) // HEAD_DIM
A_PREV_CHUNKS = 8
A_MAX_REL = 128
B_HEADS = (D_MODEL // 2) // HEAD_DIM
B_KV_HEADS = max(1, B_HEADS // 4)
B_GROUP = B_HEADS // B_KV_HEADS
B_WINDOW = 128
B_PREV_CHUNKS = B_WINDOW // CHUNK
T5_BUCKETS = 32
T5_MAX_DIST = 128
POOL_WINDOWS = (2, 4, 8, 16)
POOL_GROUPS = len(POOL_WINDOWS)
POOL_CH = D_MODEL // POOL_GROUPS
D_FF = 4 * D_MODEL
RMS_EPS = 1e-6
NEG_INF = -1e30
N_EVEN = (DEPTH + 1) // 2
N_ODD = DEPTH // 2
A_WIDTH = A_HEADS * HEAD_DIM
B_WIDTH = B_HEADS * HEAD_DIM
B_KV_WIDTH = B_KV_HEADS * HEAD_DIM
IN_PROJ_WIDTH = 3 * A_WIDTH + B_WIDTH + 2 * B_KV_WIDTH
MIX_OUT_WIDTH = A_WIDTH + B_WIDTH

kernel_name = "chunk_causal_hybrid_bandattn_pool_block"


def rms_norm(x, g):
    xf = x.astype(jnp.float32)
    ms = jnp.mean(xf * xf, axis=-1, keepdims=True)
    return (xf * lax.rsqrt(ms + RMS_EPS)).astype(x.dtype) * g


def chunk_band(t, n_prev):
    nC = t.shape[0]
    tp = jnp.pad(t, ((n_prev, 0), (0, 0), (0, 0), (0, 0)))
    band = jnp.stack([tp[j:j + nC] for j in range(n_prev + 1)], axis=1)
    return band.reshape(nC, (n_prev + 1) * CHUNK, *t.shape[2:])


def band_valid(nC, n_prev):
    c = np.arange(nC)[:, None]
    kc = np.arange((n_prev + 1) * CHUNK)[None, :] // CHUNK
    return jnp.asarray(c - n_prev + kc >= 0)


def band_rel(n_prev):
    i = np.arange(CHUNK)[:, None]
    k = np.arange((n_prev + 1) * CHUNK)[None, :]
    return i - (k - n_prev * CHUNK)


def t5_bucket(rel_kq):
    nb = T5_BUCKETS // 2
    ret = (rel_kq > 0).astype(np.int32) * nb
    n = np.abs(rel_kq)
    max_exact = nb // 2
    large = max_exact + (np.log(np.maximum(n, 1) / max_exact)
                         / math.log(T5_MAX_DIST / max_exact) * (nb - max_exact)).astype(np.int32)
    large = np.minimum(large, nb - 1)
    return ret + np.where(n < max_exact, n, large)


def band_attention(q, kb, vb, bias, valid, sink):
    s = jnp.einsum('cqngd,cknd->ngcqk', q, kb).astype(jnp.float32) * (HEAD_DIM ** -0.5)
    s = s + bias[:, :, None].astype(jnp.float32)
    s = jnp.where(valid[:, None, :], s, NEG_INF)
    m = jnp.max(s, axis=-1, keepdims=True)
    if sink is not None:
        sk = sink.astype(jnp.float32)[:, :, None, None, None]
        m = jnp.maximum(m, sk)
    p = jnp.exp(s - m)
    denom = jnp.sum(p, axis=-1, keepdims=True)
    if sink is not None:
        denom = denom + jnp.exp(sk - m)
    p = (p / denom).astype(vb.dtype)
    return jnp.einsum('ngcqk,cknd->cqngd', p, vb)


def even_mixer(h, w_in, w_out, relpos_a, sink_b, t5_table):
    bsz, s_len, _ = h.shape
    nC = s_len // CHUNK
    proj = h @ w_in
    cuts = [A_WIDTH, 2 * A_WIDTH, 3 * A_WIDTH, 3 * A_WIDTH + B_WIDTH,
            3 * A_WIDTH + B_WIDTH + B_KV_WIDTH]
    qa, ka, va, qb, kb, vb = jnp.split(proj, cuts, axis=-1)
    qa = qa.reshape(bsz, nC, CHUNK, A_HEADS, 1, HEAD_DIM)
    ka = ka.reshape(bsz, nC, CHUNK, A_HEADS, HEAD_DIM)
    va = va.reshape(bsz, nC, CHUNK, A_HEADS, HEAD_DIM)
    qb = qb.reshape(bsz, nC, CHUNK, B_KV_HEADS, B_GROUP, HEAD_DIM)
    kb = kb.reshape(bsz, nC, CHUNK, B_KV_HEADS, HEAD_DIM)
    vb = vb.reshape(bsz, nC, CHUNK, B_KV_HEADS, HEAD_DIM)

    idx_a = np.clip(band_rel(A_PREV_CHUNKS), -A_MAX_REL, A_MAX_REL) + A_MAX_REL
    bias_a = jnp.transpose(relpos_a[idx_a], (2, 0, 1))[:, None]
    valid_a = band_valid(nC, A_PREV_CHUNKS)
    idx_b = t5_bucket(-band_rel(B_PREV_CHUNKS))
    bias_b = jnp.transpose(t5_table[idx_b], (2, 0, 1)).reshape(
        B_KV_HEADS, B_GROUP, CHUNK, (B_PREV_CHUNKS + 1) * CHUNK)
    valid_b = band_valid(nC, B_PREV_CHUNKS)

    def per_sample(args):
        qa_s, ka_s, va_s, qb_s, kb_s, vb_s = args
        oa = band_attention(qa_s, chunk_band(ka_s, A_PREV_CHUNKS), chunk_band(va_s, A_PREV_CHUNKS),
                            bias_a, valid_a, None)
        ob = band_attention(qb_s, chunk_band(kb_s, B_PREV_CHUNKS), chunk_band(vb_s, B_PREV_CHUNKS),
                            bias_b, valid_b, sink_b.reshape(B_KV_HEADS, B_GROUP))
        return (oa.reshape(s_len, A_WIDTH), ob.reshape(s_len, B_WIDTH))

    oa, ob = lax.map(per_sample, (qa, ka, va, qb, kb, vb))
    return jnp.concatenate([oa, ob], axis=-1) @ w_out


def pool_mixer(h, pool_w, pool_scale):
    bsz, s_len, _ = h.shape
    hf = h.astype(jnp.float32)
    cs = jnp.cumsum(hf, axis=1)
    t = jnp.arange(s_len)
    outs = []
    for g, w in enumerate(POOL_WINDOWS):
        sl = slice(g * POOL_CH, (g + 1) * POOL_CH)
        c = cs[..., sl]
        c_prev = jnp.pad(c, ((0, 0), (w, 0), (0, 0)))[:, :s_len]
        cnt = jnp.minimum(t + 1, w).astype(jnp.float32)[:, None]
        outs.append((c - c_prev) / cnt - hf[..., sl])
    d = jnp.stack(outs, axis=2).astype(h.dtype)
    y = jnp.einsum('bsgc,gce->bsge', d, pool_w).reshape(bsz, s_len, D_MODEL)
    return y * pool_scale


def sq_relu_mlp(h, w_up, w_down):
    u = jax.nn.relu(h @ w_up)
    return (u * u) @ w_down


def setup_inputs(seed: int = 0) -> dict:
    key = jax.random.key(seed)
    ks = jax.random.split(key, 20)
    f32 = jnp.float32

    def nrm(k, shape, scale):
        return jax.random.normal(k, shape, f32) * scale

    def gain(k, shape):
        return 1.0 + 0.05 * jax.random.normal(k, shape, f32)

    return {
        "x": nrm(ks[0], (BATCH, SEQ, D_MODEL), 1.0),
        "t5_table": nrm(ks[1], (T5_BUCKETS, B_HEADS), 0.5),
        "e_norm_pre": gain(ks[2], (N_EVEN, D_MODEL)),
        "e_norm_post": gain(ks[3], (N_EVEN, D_MODEL)),
        "e_w_in": nrm(ks[4], (N_EVEN, D_MODEL, IN_PROJ_WIDTH), D_MODEL ** -0.5),
        "e_w_out": nrm(ks[5], (N_EVEN, MIX_OUT_WIDTH, D_MODEL), MIX_OUT_WIDTH ** -0.5),
        "e_relpos_a": nrm(ks[6], (N_EVEN, 2 * A_MAX_REL + 1, A_HEADS), 0.5),
        "e_sink_b": nrm(ks[7], (N_EVEN, B_HEADS), 0.5),
        "o_norm_pre": gain(ks[8], (N_ODD, D_MODEL)),
        "o_norm_post": gain(ks[9], (N_ODD, D_MODEL)),
        "o_pool_w": nrm(ks[10], (N_ODD, POOL_GROUPS, POOL_CH, POOL_CH), POOL_CH ** -0.5),
        "o_pool_scale": gain(ks[11], (N_ODD, D_MODEL)),
        "mlp_norm_pre": gain(ks[12], (DEPTH, D_MODEL)),
        "mlp_norm_post": gain(ks[13], (DEPTH, D_MODEL)),
        "mlp_w_up": nrm(ks[14], (DEPTH, D_MODEL, D_FF), D_MODEL ** -0.5),
        "mlp_w_down": nrm(ks[15], (DEPTH, D_FF, D_MODEL), D_FF ** -0.5),
    }


def reference(x, t5_table, e_norm_pre, e_norm_post, e_w_in, e_w_out, e_relpos_a, e_sink_b,
              o_norm_pre, o_norm_post, o_pool_w, o_pool_scale,
              mlp_norm_pre, mlp_norm_post, mlp_w_up, mlp_w_down):
    h = x
    for layer in range(DEPTH):
        i = layer // 2
        if layer % 2 == 0:
            y = even_mixer(rms_norm(h, e_norm_pre[i]), e_w_in[i], e_w_out[i],
                           e_relpos_a[i], e_sink_b[i], t5_table)
            h = h + rms_norm(y, e_norm_post[i])
        else:
            y = pool_mixer(rms_norm(h, o_norm_pre[i]), o_pool_w[i], o_pool_scale[i])
            h = h + rms_norm(y, o_norm_post[i])
        y = sq_relu_mlp(rms_norm(h, mlp_norm_pre[layer]), mlp_w_up[layer], mlp_w_down[layer])
        h = h + rms_norm(y, mlp_norm_post[layer])
    return h
```

```python
import math
from contextlib import ExitStack
import numpy as np
import concourse.bass as bass
import concourse.mybir as mybir
from concourse.bass_utils import run_bass_kernel_spmd

F32 = mybir.dt.float32
BF16 = mybir.dt.bfloat16
AF = mybir.ActivationFunctionType
ALU = mybir.AluOpType

NCORES = 8
D = 1024
KT = 8
SEQ = 8192
OWN = 4096
HALO_KV = 512
HALO_Q = 128
XROWS = OWN + HALO_KV + HALO_Q
MT_TILES = [6, 6, 6, 5, 5, 5]
TMAX = 128 * max(MT_TILES)
NSLOT = 4
EPS = 1e-6

ENGS = ['pe', 'act', 'dve', 'pool', 'sp']
SAME_SYNC = True


class Prog:
    def __init__(self, nc, es):
        self.nc = nc
        self.es = es
        self.ops = []
        self.lastw = {}
        self.readers = {}
        self.chan_cnt = {}
        self.final_chans = []
        self.fences = {}
        self.last_eng = {}
        self.last_chan = {}

    def fence(self, prefixes):
        d = set(self.last_eng.values()) | set(self.last_chan.values())
        for p in prefixes:
            self.fences[p] = d

    def op(self, eng, fn, r=(), w=(), dma=None):
        i = len(self.ops)
        deps = set()
        for x in r:
            if x in self.lastw:
                deps.add(self.lastw[x])
        for x in w:
            if x in self.lastw:
                deps.add(self.lastw[x])
            deps.update(self.readers.get(x, ()))
        for x in list(r) + list(w):
            p = x[0] if isinstance(x, tuple) else x
            if p in self.fences:
                deps.update(self.fences[p])
        for x in r:
            self.readers.setdefault(x, []).append(i)
        for x in w:
            self.lastw[x] = i
            self.readers[x] = []
        dval = None
        if dma is not None:
            self.chan_cnt[dma] = self.chan_cnt.get(dma, 0) + 1
            dval = self.chan_cnt[dma]
            self.last_chan[dma] = i
        else:
            self.last_eng[eng] = i
        self.ops.append(dict(eng=eng, fn=fn, deps=deps, dma=dma, dval=dval, idx=i, cval=None))
        return i

    def build(self):
        nc, es = self.nc, self.es
        ops = self.ops
        eng_ops = {e: [o for o in ops if o['eng'] == e] for e in ENGS}
        need = set()
        for o in ops:
            for d in o['deps']:
                do = ops[d]
                if do['dma'] is not None:
                    continue
                if do['eng'] != o['eng'] or (SAME_SYNC and o['eng'] != 'pe'):
                    need.add(d)
        for e, lst in eng_ops.items():
            c = 0
            for o in lst:
                if o['dma'] is None and o['idx'] in need:
                    c += 1
                    o['cval'] = c
        esem = {e: es.enter_context(nc.semaphore('s_' + e)) for e in ENGS}
        csem = {c: es.enter_context(nc.semaphore('c_%d' % k)) for k, c in enumerate(self.chan_cnt)}
        self.stats = {e: len(l) for e, l in eng_ops.items()}
        nwaits = {e: 0 for e in ENGS}

        def emit(E, eng):
            known = {}
            for o in eng_ops[E]:
                waits = {}
                for d in o['deps']:
                    do = ops[d]
                    if do['dma'] is not None:
                        key = ('c', do['dma'])
                        val = 16 * (self.chan_cnt[do['dma']] if do['dma'] == 'const' else do['dval'])
                    else:
                        if do['eng'] == E and (E == 'pe' or not SAME_SYNC):
                            continue
                        key = ('e', do['eng'])
                        val = do['cval']
                    if waits.get(key, 0) < val:
                        waits[key] = val
                for key, val in waits.items():
                    if known.get(key, 0) < val:
                        sem = csem[key[1]] if key[0] == 'c' else esem[key[1]]
                        eng.wait_ge(sem, val)
                        known[key] = val
                        nwaits[E] += 1
                ins = o['fn'](eng)
                if o['dma'] is not None:
                    ins.then_inc(csem[o['dma']], 16)
                elif o['cval'] is not None:
                    ins.then_inc(esem[E], 1)
            if E == 'sp':
                for c in self.final_chans:
                    eng.wait_ge(csem[c], 16 * self.chan_cnt[c])

        block = es.enter_context(nc.Block())

        @block.tensor
        def _(e):
            emit('pe', e)

        @block.scalar
        def _(e):
            emit('act', e)

        @block.vector
        def _(e):
            emit('dve', e)

        @block.gpsimd
        def _(e):
            emit('pool', e)

        @block.sync
        def _(e):
            emit('sp', e)
        self.nwaits = nwaits


C_QA, C_QB, C_KA, C_VA, C_KVB, C_WO0, C_WO1 = 0, 1, 2, 3, 4, 5, 6
C_UP0, C_DN0, C_POOL, C_UP1, C_DN1 = 7, 15, 23, 24, 32
NCHUNK = 40
G_EPRE, G_EPOST, G_M0PRE, G_M0POST, G_OPRE, G_OPOST, G_M1PRE, G_M1POST, G_PSCALE = range(9)
EA_W = 640
EB_W = 256


class _Stop(Exception):
    pass


def build_program(debug=False, stop=None):
    nc = bass.Bass("TRN2", target_bir_lowering=False)
    x = nc.dram_tensor("x", [XROWS, D], F32, kind="ExternalInput").ap()
    wch = nc.dram_tensor("wch", [NCHUNK, 128, 4096], F32, kind="ExternalInput").ap()
    etab_d = nc.dram_tensor("etab", [128, 8 * (EA_W + EB_W)], F32, kind="ExternalInput").ap()
    gains_d = nc.dram_tensor("gains", [128, 9 * 8], F32, kind="ExternalInput").ap()
    vbias_d = nc.dram_tensor("vbias", [128, 6], F32, kind="ExternalInput").ap()
    sink_d = nc.dram_tensor("sinkrep", [128, 256], F32, kind="ExternalInput").ap()
    pmask_d = nc.dram_tensor("pmask", [128, 16], F32, kind="ExternalInput").ap()
    rcnt_d = nc.dram_tensor("rcnt", [128, 64], F32, kind="ExternalInput").ap()
    ident_d = nc.dram_tensor("ident", [128, 128], F32, kind="ExternalInput").ap()
    out = nc.dram_tensor("out", [OWN, D], F32, kind="ExternalOutput").ap()

    es = ExitStack()
    with es:
        P = Prog(nc, es)

        def sb(name, shape, dt):
            return es.enter_context(nc.sbuf_tensor(name, shape, dt))

        hT = sb("hT", [128, KT, TMAX], F32)
        yT = sb("yT", [128, KT, TMAX], F32)
        actT = sb("actT", [128, KT, TMAX], BF16)
        KTb = sb("KTb", [128, 5, 512 + TMAX], BF16)
        Vb = sb("Vb", [128, 4 + TMAX // 128, 640], BF16)
        wb = [sb("wb%d" % i, [128, 4096], BF16) for i in range(NSLOT)]
        xs = [sb("xs%d" % i, [128, D], F32) for i in range(2)]
        sqb = sb("sqb", [128, KT, 512], BF16)
        rstd = sb("rstd", [128, 512], F32)
        tmpf = [sb("tmpf%d" % i, [128, 512], F32) for i in range(2)]
        ident = sb("identS", [128, 128], F32)
        ones = sb("ones", [128, 128], BF16)
        gains = sb("gainsS", [128, 9, 8], F32)
        vbias = sb("vbiasS", [128, 6], F32)
        sinkE = sb("sinkE", [128, 256], F32)
        pmask = sb("pmaskS", [128, 16], F32)
        rcnt = sb("rcntS", [128, 4, 16], F32)
        epsc = sb("epsc", [128, 1], F32)
        carry = sb("carry", [128, KT, 16], F32)
        rD = sb("rD", [128, 256], F32)
        NR = max(4 * TMAX + 2 * 7 * 256 + 8 * (EA_W + EB_W) + 3 * 512, 16 * TMAX, 12 * (TMAX + 16))
        R = sb("R", [128, NR], F32)
        off = [0]

        def carve(nwords):
            a = off[0]
            off[0] += nwords
            assert off[0] <= NR, (off[0], NR)
            return R[:, a:a + nwords]

        uT = R[:, 0:16 * TMAX].bitcast(BF16).rearrange("p (f t) -> p f t", f=32)
        off[0] = 0
        QT = carve(4 * TMAX).bitcast(BF16).rearrange("p (k t) -> p k t", k=8)
        PT = [carve(7 * 256).bitcast(BF16).rearrange("p (k t) -> p k t", k=7) for _ in range(2)]
        Ea = carve(8 * EA_W).rearrange("p (h j) -> p h j", h=8)
        Eb = carve(8 * EB_W).rearrange("p (h j) -> p h j", h=8)
        sc = [carve(512) for _ in range(3)]
        off[0] = 0
        hn1 = carve(8 * (TMAX + 16)).rearrange("p (k t) -> p k t", k=8)
        sA = carve(2 * (TMAX + 16)).rearrange("p (k t) -> p k t", k=2)
        sB = carve(2 * (TMAX + 16)).rearrange("p (k t) -> p k t", k=2)

        ps = [es.enter_context(nc.psum_tensor("ps%d" % i, [128, 512], F32)) for i in range(8)]
        bankc = [0]

        def nextbank():
            b = bankc[0] % 8
            bankc[0] += 1
            return b

        evc = [0]
        scn = [0]

        def evac_eng():
            evc[0] += 1
            return 'act' if evc[0] % 2 else 'dve'

        def g(name, c0, n):
            return [(name, i) for i in range(c0 // 64, (c0 + n + 63) // 64)]

        def MM(o, lhsT, rhs, start, stop, r, w):
            P.op('pe', lambda e: e.matmul(o, lhsT=lhsT, rhs=rhs, start=start, stop=stop), r=r, w=w)

        def TR(o, in_, r, w):
            P.op('pe', lambda e: e.transpose(out=o, in_=in_, identity=ident[:]), r=list(r) + ['ident'], w=w)

        def ACT(o, in_, func, r, w, bias=None, scale=None):
            kw = {}
            if bias is not None:
                kw['bias'] = bias
            if scale is not None:
                kw['scale'] = scale
            P.op('act', lambda e: e.activation(out=o, in_=in_, func=func, **kw), r=r, w=w)

        def COPY(eng, o, in_, r, w):
            if eng == 'act':
                ACT(o, in_, AF.Copy, r, w)
            else:
                P.op(eng, lambda e: e.tensor_copy(out=o, in_=in_), r=r, w=w)

        def TT(eng, o, in0, in1, op, r, w):
            P.op(eng, lambda e: e.tensor_tensor(out=o, in0=in0, in1=in1, op=op), r=r, w=w)

        def STT(eng, o, in0, scalar, in1, op0, op1, r, w):
            P.op(eng, lambda e: e.scalar_tensor_tensor(out=o, in0=in0, scalar=scalar, in1=in1, op0=op0, op1=op1), r=r, w=w)

        def TS(eng, o, in0, s1, op0, r, w):
            P.op(eng, lambda e: e.tensor_scalar(out=o, in0=in0, scalar1=s1, scalar2=None, op0=op0), r=r, w=w)

        def RECIP(o, in_, r, w):
            P.op('dve', lambda e: e.reciprocal(out=o, in_=in_), r=r, w=w)

        def DMA(eng, o, in_, r, w, chan):
            P.op(eng, lambda e: e.dma_start(out=o, in_=in_), r=r, w=w, dma=chan)

        DMA('sp', ident[:], ident_d[:, :], [], ['ident'], 'const')
        if stop == 'c1':
            P.build()
            return nc
        DMA('sp', gains[:], gains_d.rearrange("p (g k) -> p g k", g=9), [], ['gains'], 'const')
        DMA('sp', vbias[:], vbias_d[:, :], [], ['vbias'], 'const')
        DMA('sp', sinkE[:], sink_d[:, :], [], ['sinkE'], 'const')
        DMA('sp', pmask[:], pmask_d[:, :], [], ['pmask'], 'const')
        DMA('sp', rcnt[:], rcnt_d.rearrange("p (g t) -> p g t", g=4), [], ['rcnt'], 'const')
        if stop == 'c2':
            P.build()
            return nc
        P.op('pool', lambda e: e.memset(ones[:], 1.0), w=['ones'])
        P.op('pool', lambda e: e.memset(epsc[:], EPS), w=['epsc'])
        P.op('pool', lambda e: e.memset(carry[:], 0.0), w=['carry'])
        if stop == 'c3':
            P.build()
            return nc
        ACT(sinkE[:], sinkE[:], AF.Exp, ['sinkE'], ['sinkE'])
        if stop == 'consts':
            P.build()
            return nc

        per_mt = ([C_QA, C_QB, C_KA, C_VA, C_KVB, C_WO0, C_WO1] + [C_UP0 + i for i in range(8)] +
                  [C_DN0 + i for i in range(8)] + [C_POOL] + [C_UP1 + i for i in range(8)] +
                  [C_DN1 + i for i in range(8)])
        stream = [C_KA, C_VA, C_KVB] + per_mt * len(MT_TILES)
        wst = dict(cur=-1, nxt=0)

        def acquire(expect):
            wst['cur'] += 1
            i = wst['cur']
            assert stream[i] == expect, (i, stream[i], expect)
            while wst['nxt'] < len(stream) and wst['nxt'] < i + NSLOT:
                j = wst['nxt']
                DMA('pool', wb[j % NSLOT][:], wch[stream[j]], [], [('w', j % NSLOT)], ('w', j % NSLOT))
                wst['nxt'] += 1
            return i % NSLOT

        xsc = [0]

        def load_x(row0, ntiles, col0):
            for i in range(ntiles):
                s = xsc[0] % 2
                xsc[0] += 1
                DMA('sp', xs[s][:], x[row0 + 128 * i: row0 + 128 * (i + 1), :], [], [('xs', s)], ('xs', s))
                for half in range(2):
                    b = nextbank()
                    for q in range(4):
                        kt = half * 4 + q
                        TR(ps[b][:, q * 128:(q + 1) * 128], xs[s][:, kt * 128:(kt + 1) * 128], [('xs', s)], [('ps', b)])
                    c = col0 + 128 * i
                    COPY(evac_eng(), hT[:, half * 4:half * 4 + 4, c:c + 128],
                         ps[b][:].rearrange("p (k t) -> p k t", k=4), [('ps', b)], g('hT', c, 128))

        def norm_block(src, sname, c0, n, gi, mode, dst=None, dname=None, dc0=0):
            ACT(sqb[:, :, 0:n], src[:, :, c0:c0 + n], AF.Square, g(sname, c0, n), ['sqb'])
            b = nextbank()
            for kt in range(KT):
                MM(ps[b][:, 0:n], ones[:, :], sqb[:, kt, 0:n], kt == 0, kt == KT - 1, ['sqb', 'ones'], [('ps', b)])
            ACT(rstd[:, 0:n], ps[b][:, 0:n], AF.Sqrt, [('ps', b), 'epsc'], ['rstd'], bias=epsc[:], scale=1.0 / D)
            RECIP(rstd[:, 0:n], rstd[:, 0:n], ['rstd'], ['rstd'])
            for kt in range(KT):
                if mode == 'pre':
                    STT('dve', dst[:, kt, dc0:dc0 + n], src[:, kt, c0:c0 + n], gains[:, gi, kt:kt + 1], rstd[:, 0:n],
                        ALU.mult, ALU.mult, g(sname, c0, n) + ['rstd', 'gains'], g(dname, dc0, n))
                else:
                    t = tmpf[kt % 2]
                    STT('dve', t[:, 0:n], src[:, kt, c0:c0 + n], gains[:, gi, kt:kt + 1], rstd[:, 0:n],
                        ALU.mult, ALU.mult, g(sname, c0, n) + ['rstd', 'gains'], [('tmpf', kt % 2)])
                    TT('pool', hT[:, kt, c0:c0 + n], hT[:, kt, c0:c0 + n], t[:, 0:n], ALU.add,
                       [('tmpf', kt % 2)] + g('hT', c0, n), g('hT', c0, n))

        def proj_fm(slot, wc0, blocks, evac):
            for (c0, n) in blocks:
                b = nextbank()
                for kt in range(KT):
                    MM(ps[b][:, 0:n], wb[slot][:, kt * 512 + wc0: kt * 512 + wc0 + 128], actT[:, kt, c0:c0 + n],
                       kt == 0, kt == KT - 1, [('w', slot)] + g('actT', c0, n), [('ps', b)])
                evac(b, c0, n)

        def kv_proj(tiles0, ntile, blocks):
            kcol = 128 * tiles0
            s = acquire(C_KA)
            for ft in range(4):
                proj_fm(s, ft * 128, blocks,
                        lambda b, c0, n, ft=ft: COPY(evac_eng(), KTb[:, ft, kcol + c0:kcol + c0 + n], ps[b][:, 0:n],
                                                     [('ps', b)], g('KT', kcol + c0, n)))
            s = acquire(C_VA)
            for tt in range(ntile):
                b = nextbank()
                for kt in range(KT):
                    MM(ps[b][:, :], actT[:, kt, tt * 128:(tt + 1) * 128], wb[s][:, kt * 512:(kt + 1) * 512],
                       kt == 0, kt == KT - 1, [('w', s)] + g('actT', tt * 128, 128), [('ps', b)])
                COPY(evac_eng(), Vb[:, tiles0 + tt, 0:512], ps[b][:, :], [('ps', b)], [('V', tiles0 + tt)])
            s = acquire(C_KVB)
            proj_fm(s, 0, blocks,
                    lambda b, c0, n: COPY(evac_eng(), KTb[:, 4, kcol + c0:kcol + c0 + n], ps[b][:, 0:n],
                                          [('ps', b)], g('KT', kcol + c0, n)))
            for tt in range(ntile):
                b = nextbank()
                for kt in range(KT):
                    MM(ps[b][:, 0:128], actT[:, kt, tt * 128:(tt + 1) * 128], wb[s][:, kt * 512 + 128: kt * 512 + 256],
                       kt == 0, kt == KT - 1, [('w', s)] + g('actT', tt * 128, 128), [('ps', b)])
                COPY(evac_eng(), Vb[:, tiles0 + tt, 512:640], ps[b][:, 0:128], [('ps', b)], [('V', tiles0 + tt)])

        def key_tiles(lo, hi):
            res = []
            kc = (lo // 128) * 128
            while kc < hi:
                res.append((kc, max(kc, lo) - kc, min(kc + 128, hi) - kc))
                kc += 128
            return res

        def attn_part(ms, q0, pb, tiles, pt0, is_b):
            E = Eb if is_b else Ea
            bA = bB = None
            for ti, (kc0, p0, p1) in enumerate(tiles):
                if ti % 2 == 0:
                    bA, bB = nextbank(), nextbank()
                co = (ti % 2) * 256
                kr = g('KT', kc0 + p0, p1 - p0)
                if is_b:
                    for n2 in range(2):
                        bb = bA if n2 == 0 else bB
                        MM(ps[bb][p0:p1, co:co + 256], KTb[64 * n2:64 * n2 + 64, 4, kc0 + p0:kc0 + p1],
                           QT[64 * n2:64 * n2 + 64, 4:8, q0:q0 + 64], True, True, kr + g('QT', q0, 64), [('ps', bb)])
                else:
                    for h in range(8):
                        hp, j = 64 * (h % 2), h // 2
                        bb = bA if h % 2 == 0 else bB
                        MM(ps[bb][p0:p1, co + j * 64:co + (j + 1) * 64], KTb[hp:hp + 64, j, kc0 + p0:kc0 + p1],
                           QT[hp:hp + 64, j, q0:q0 + 64], True, True, kr + g('QT', q0, 64), [('ps', bb)])
                    chk('a_s')
                joff = q0 + 512 - kc0
                tok0 = ms + kc0 - 512
                vcol = (tok0 + 640) // 128 if tok0 < 0 else 5
                for par in range(2):
                    bb = bA if par == 0 else bB
                    scn[0] += 1
                    s = scn[0] % 3
                    if is_b:
                        esl = E[p0:p1, 4 * par:4 * par + 4, joff:joff + 64]
                        pto = PT[pb][p0:p1, pt0 + ti, par * 256:(par + 1) * 256]
                    else:
                        esl = E[p0:p1, :, joff:joff + 64].rearrange("p (j two) q -> p j two q", two=2)[:, :, par, :]
                        pto = PT[pb][p0:p1, pt0 + ti, :].rearrange("p (j two q) -> p j two q", two=2, q=64)[:, :, par, :]
                    TT('dve', sc[s][p0:p1, 0:256].rearrange("p (h q) -> p h q", h=4),
                       ps[bb][p0:p1, co:co + 256].rearrange("p (h q) -> p h q", h=4), esl, ALU.add,
                       [('ps', bb), 'E'], [('sc', s)])
                    ACT(pto, sc[s][p0:p1, 0:256].rearrange("p (h q) -> p h q", h=4) if not is_b else sc[s][p0:p1, 0:256],
                        AF.Exp, [('sc', s), 'vbias'], [('PT', pb, pt0 + ti)], bias=vbias[p0:p1, vcol:vcol + 1])
                if not is_b and ti == 0: chk('a_exp')
            if not is_b: chk('a_tiles')
            bo = nextbank()
            nt = len(tiles)
            for h in range(8):
                hp, j = 64 * (h % 2), h // 2
                vc0 = (512 + 64 * (h // 4)) if is_b else h * 64
                for ti, (kc0, p0, p1) in enumerate(tiles):
                    vt = kc0 // 128
                    MM(ps[bo][hp:hp + 64, j * 64:(j + 1) * 64], Vb[p0:p1, vt, vc0:vc0 + 64],
                       PT[pb][p0:p1, pt0 + ti, h * 64:(h + 1) * 64], ti == 0, ti == nt - 1,
                       [('V', vt), ('PT', pb, pt0 + ti)], [('ps', bo)])
                if not is_b and h == 0: chk('a_pv0')
                if not is_b and h == 1: chk('a_pv1')
            if not is_b: chk('a_pv')
            for hpar in range(2):
                for ti, (kc0, p0, p1) in enumerate(tiles):
                    rhs = PT[pb][p0:p1, pt0 + ti, :].rearrange("p (j two q) -> p j two q", two=2, q=64)[:, :, hpar, :]
                    MM(ps[bo][64 * hpar:64 * hpar + 64, 256:512], ones[p0:p1, 0:64], rhs, ti == 0, ti == nt - 1,
                       [('PT', pb, pt0 + ti), 'ones'], [('ps', bo)])
            if not is_b: chk('a_d')
            if is_b:
                TT('dve', rD[:, :], ps[bo][:, 256:512], sinkE[:, :], ALU.add, [('ps', bo), 'sinkE'], ['rD'])
            else:
                TS('dve', rD[:, :], ps[bo][:, 256:512], 1e-30, ALU.max, [('ps', bo)], ['rD'])
            RECIP(rD[:, :], rD[:, :], ['rD'], ['rD'])
            a0 = 4 if is_b else 0
            TT('dve', actT[:, a0:a0 + 4, q0:q0 + 64], ps[bo][:, 0:256].rearrange("p (j q) -> p j q", j=4),
               rD[:, :].rearrange("p (j q) -> p j q", j=4), ALU.mult, [('ps', bo), 'rD'], g('actT', q0, 64))
            if not is_b: chk('a_norm')
            if is_b: chk('b_all')

        def mlp(layer, T, blocks, gpre, gpost, c_up, c_dn):
            P.fence(['uT', 'rl'])
            for (c0, n) in blocks:
                norm_block(hT, 'hT', c0, n, gpre, 'pre', actT, 'actT', c0)
            for fc in range(8):
                s = acquire(c_up + fc)
                for fq in range(4):
                    def ev(b, c0, n, f=fc * 4 + fq):
                        k = f % 2
                        ACT(tmpf[k][:, 0:n], ps[b][:, 0:n], AF.Relu, [('ps', b)], [('tmpf', k)])
                        TT('dve', uT[:, f, c0:c0 + n], tmpf[k][:, 0:n], ps[b][:, 0:n], ALU.mult,
                           [('tmpf', k), ('ps', b)], g('uT', c0, n))
                    proj_fm(s, fq * 128, blocks, ev)
            for dt in range(8):
                s = acquire(c_dn + dt)
                for (c0, n) in blocks:
                    b = nextbank()
                    for ft in range(32):
                        MM(ps[b][:, 0:n], wb[s][:, ft * 128:(ft + 1) * 128], uT[:, ft, c0:c0 + n], ft == 0, ft == 31,
                           [('w', s)] + g('uT', c0, n), [('ps', b)])
                    COPY(evac_eng(), yT[:, dt, c0:c0 + n], ps[b][:, 0:n], [('ps', b)], g('yT', c0, n))
            for (c0, n) in blocks:
                norm_block(yT, 'yT', c0, n, gpost, 'post')

        def chk(tag):
            if stop is not None and tag == stop:
                raise _Stop()

        try:
            if stop == 'x1':
                DMA('sp', xs[0][:], x[0:128, :], [], [('xs', 0)], ('xs', 0))
                raise _Stop()
            if stop == 'x2':
                DMA('sp', xs[0][:], x[0:128, :], [], [('xs', 0)], ('xs', 0))
                TR(ps[0][:, 0:128], xs[0][:, 0:128], [('xs', 0)], [('ps', 0)])
                raise _Stop()
            if stop == 'x3':
                DMA('sp', xs[0][:], x[0:128, :], [], [('xs', 0)], ('xs', 0))
                TR(ps[0][:, 0:128], xs[0][:, 0:128], [('xs', 0)], [('ps', 0)])
                COPY('dve', hT[:, 0, 0:128], ps[0][:, 0:128], [('ps', 0)], g('hT', 0, 128))
                raise _Stop()
            if stop in ('t2', 't3'):
                load_x(0, int(stop[1]), 0)
                raise _Stop()
            if stop == 'x4':
                load_x(0, 1, 0)
                raise _Stop()
            load_x(0, 4, 0)
            chk('pre_load')
            norm_block(hT, 'hT', 0, 512, G_EPRE, 'pre', actT, 'actT', 0)
            chk('pre_norm')
            kv_proj(0, 4, [(0, 512)])
            chk('pre')

            ms = -HALO_Q
            orow = 0
            for mi, ntile in enumerate(MT_TILES):
                T = 128 * ntile
                half = T // 2
                blocks = [(0, half), (half, T - half)]
                load_x(ms + HALO_KV + HALO_Q, ntile, 0)
                P.fence(['QT', 'PT', 'E', 'sc'])
                DMA('sp', R[:, 4 * TMAX + 2 * 7 * 256: 4 * TMAX + 2 * 7 * 256 + 8 * (EA_W + EB_W)], etab_d[:, :], [], ['E'], 'etab')
                for (c0, n) in blocks:
                    norm_block(hT, 'hT', c0, n, G_EPRE, 'pre', actT, 'actT', c0)
                for ci, cid in enumerate((C_QA, C_QB)):
                    s = acquire(cid)
                    for ft in range(4):
                        proj_fm(s, ft * 128, blocks,
                                lambda b, c0, n, ft=ft, ci=ci: ACT(QT[:, 4 * ci + ft, c0:c0 + n], ps[b][:, 0:n], AF.Copy,
                                                                   [('ps', b)], g('QT', c0, n), scale=0.125))
                kv_proj(4, ntile, blocks)
                chk('qkv')
                for ck in range(T // 64):
                    q0 = 64 * ck
                    pb = ck % 2
                    attn_part(ms, q0, pb, key_tiles(q0, q0 + 576), 0, False)
                    attn_part(ms, q0, pb, key_tiles(q0 + 384, q0 + 576), 5, True)
                chk('attn')
                for dh in range(2):
                    s = acquire(C_WO0 + dh)
                    for dq in range(4):
                        proj_fm(s, dq * 128, blocks,
                                lambda b, c0, n, dt=dh * 4 + dq: COPY(evac_eng(), yT[:, dt, c0:c0 + n], ps[b][:, 0:n],
                                                                      [('ps', b)], g('yT', c0, n)))
                for (c0, n) in blocks:
                    norm_block(yT, 'yT', c0, n, G_EPOST, 'post')
                if mi + 1 < len(MT_TILES):
                    COPY('pool', KTb[:, :, 0:512], KTb[:, :, T:T + 512], g('KT', T, 512), g('KT', 0, 512))
                    COPY('pool', Vb[:, 0:4, :], Vb[:, ntile:ntile + 4, :], [('V', ntile + i) for i in range(4)],
                         [('V', i) for i in range(4)])
                chk('mixer')
                mlp(0, T, blocks, G_M0PRE, G_M0POST, C_UP0, C_DN0)
                chk('mlp0')
                P.fence(['hn1', 'sA', 'sB'])
                COPY('pool', hn1[:, :, 0:16], carry[:, :, :], ['carry'], [('hn1', 0)])
                for (c0, n) in blocks:
                    norm_block(hT, 'hT', c0, n, G_OPRE, 'pre', hn1, 'hn1', 16 + c0)
                if mi + 1 < len(MT_TILES):
                    COPY('pool', carry[:, :, :], hn1[:, :, T:T + 16], g('hn1', T, 16), ['carry'])
                if mi == 0:
                    for kt in range(KT):
                        TT('pool', hn1[:, kt, 16 + 112:16 + 128], hn1[:, kt, 16 + 112:16 + 128], pmask[:, :], ALU.mult,
                           g('hn1', 128, 16) + ['pmask'], g('hn1', 128, 16))
                W = T + 16
                hr = g('hn1', 0, W)
                for gi in range(4):
                    k0 = 2 * gi
                    TT('pool', sA[:, :, 1:W], hn1[:, k0:k0 + 2, 1:W], hn1[:, k0:k0 + 2, 0:W - 1], ALU.add, hr, ['sA'])
                    cur, oth, sh = sA, sB, 2
                    cn, on = 'sA', 'sB'
                    for lvl in range(gi):
                        lo = 2 * sh - 1
                        TT('pool', oth[:, :, lo:W], cur[:, :, lo:W], cur[:, :, lo - sh:W - sh], ALU.add, [cn], [on])
                        cur, oth, cn, on = oth, cur, on, cn
                        sh *= 2
                    wsz = 2 ** (gi + 1)
                    STT('dve', actT[:, k0:k0 + 2, 0:T], cur[:, :, 16:W], 1.0 / wsz, hn1[:, k0:k0 + 2, 16:W],
                        ALU.mult, ALU.subtract, [cn] + hr, g('actT', 0, T))
                    if mi == 0:
                        for k in range(2):
                            TT('pool', tmpf[0][:, 0:16], cur[:, k, 16 + 128:16 + 144], rcnt[:, gi, :], ALU.mult,
                               [cn, 'rcnt'], [('tmpf', 0)])
                            TT('pool', actT[:, k0 + k, 128:144], tmpf[0][:, 0:16], hn1[:, k0 + k, 16 + 128:16 + 144],
                               ALU.subtract, [('tmpf', 0)] + hr, g('actT', 128, 16))
                s = acquire(C_POOL)
                for gi in range(4):
                    for et in range(2):
                        for (c0, n) in blocks:
                            b = nextbank()
                            for ct in range(2):
                                wc = gi * 512 + ct * 256 + et * 128
                                MM(ps[b][:, 0:n], wb[s][:, wc:wc + 128], actT[:, 2 * gi + ct, c0:c0 + n], ct == 0, ct == 1,
                                   [('w', s)] + g('actT', c0, n), [('ps', b)])
                            ACT(yT[:, 2 * gi + et, c0:c0 + n], ps[b][:, 0:n], AF.Copy, [('ps', b), 'gains'], g('yT', c0, n),
                                scale=gains[:, G_PSCALE, 2 * gi + et:2 * gi + et + 1])
                for (c0, n) in blocks:
                    norm_block(yT, 'yT', c0, n, G_OPOST, 'post')
                chk('pool')
                mlp(1, T, blocks, G_M1PRE, G_M1POST, C_UP1, C_DN1)
                chk('mlp1')
                for tt in range(ntile):
                    if ms + 128 * tt < 0:
                        continue
                    s = xsc[0] % 2
                    xsc[0] += 1
                    for hf in range(2):
                        b = nextbank()
                        for q in range(4):
                            kt = hf * 4 + q
                            TR(ps[b][:, q * 128:(q + 1) * 128], hT[:, kt, tt * 128:(tt + 1) * 128], g('hT', tt * 128, 128), [('ps', b)])
                        COPY(evac_eng(), xs[s][:, hf * 512:(hf + 1) * 512], ps[b][:, :], [('ps', b)], [('xs', s)])
                    DMA('sp', out[orow:orow + 128, :], xs[s][:], [('xs', s)], [], ('xs', s))
                    orow += 128
                ms += T
            assert orow == OWN and wst['cur'] == len(stream) - 1
        except _Stop:
            pass
        P.final_chans = [c for c in [('xs', 0), ('xs', 1)] if c in P.chan_cnt]
        P.build()
        if debug:
            print("ops", P.stats, "waits", P.nwaits)
    return nc


def t5_bucket(rel_kq):
    nb = 16
    ret = (rel_kq > 0).astype(np.int32) * nb
    n = np.abs(rel_kq)
    max_exact = nb // 2
    large = max_exact + (np.log(np.maximum(n, 1) / max_exact) / math.log(128 / max_exact) * (nb - max_exact)).astype(np.int32)
    large = np.minimum(large, nb - 1)
    return ret + np.where(n < max_exact, n, large)


def pack_weights(e_w_in, e_w_out, o_pool_w, mlp_w_up, mlp_w_down):
    ch = np.zeros((NCHUNK, 128, 4096), np.float32)

    def km(w):
        C = w.shape[1]
        return w.reshape(8, 128, C).transpose(1, 0, 2).reshape(128, 8 * C)

    wi = e_w_in[0]
    qa, ka, va = wi[:, 0:512], wi[:, 512:1024], wi[:, 1024:1536]
    qb, kb, vb = wi[:, 1536:2048], wi[:, 2048:2176], wi[:, 2176:2304]
    qbp = np.concatenate([np.concatenate([qb[:, j * 64:(j + 1) * 64], qb[:, (4 + j) * 64:(5 + j) * 64]], axis=1)
                          for j in range(4)], axis=1)
    ch[C_QA] = km(qa)
    ch[C_QB] = km(qbp)
    ch[C_KA] = km(ka)
    ch[C_VA] = km(va)
    kvb = np.zeros((1024, 512), np.float32)
    kvb[:, 0:128] = kb
    kvb[:, 128:256] = vb
    ch[C_KVB] = km(kvb)
    wo = e_w_out[0]
    for dh in range(2):
        ch[C_WO0 + dh] = km(wo[:, dh * 512:(dh + 1) * 512])
    for l in range(2):
        cu = C_UP0 if l == 0 else C_UP1
        cd = C_DN0 if l == 0 else C_DN1
        for fc in range(8):
            ch[cu + fc] = km(mlp_w_up[l][:, fc * 512:(fc + 1) * 512])
        for dt in range(8):
            wd = mlp_w_down[l][:, dt * 128:(dt + 1) * 128]
            ch[cd + dt] = wd.reshape(32, 128, 128).transpose(1, 0, 2).reshape(128, 4096)
    pw = o_pool_w[0]
    pp = pw.reshape(4, 2, 128, 256).transpose(2, 0, 1, 3).reshape(128, 2048)
    ch[C_POOL][:, 0:2048] = pp
    return ch


_NC_CACHE = {}


def prepare_inputs(x, t5_table, e_norm_pre, e_norm_post, e_w_in, e_w_out, e_relpos_a, e_sink_b,
           o_norm_pre, o_norm_post, o_pool_w, o_pool_scale,
           mlp_norm_pre, mlp_norm_post, mlp_w_up, mlp_w_down):
    f = lambda a: np.ascontiguousarray(np.asarray(a, dtype=np.float32))
    x = f(x)
    wch = pack_weights(f(e_w_in), f(e_w_out), f(o_pool_w), f(mlp_w_up), f(mlp_w_down))
    p = np.arange(128)[:, None]
    ja = np.arange(EA_W)[None, :]
    idx_a = np.clip(ja - p, -128, 128) + 128
    Ea = f(e_relpos_a)[0][idx_a]
    jb = np.arange(EB_W)[None, :]
    idx_b = t5_bucket(-(jb - p))
    Eb = f(t5_table)[idx_b]
    etab = np.concatenate([Ea.transpose(0, 2, 1).reshape(128, -1), Eb.transpose(0, 2, 1).reshape(128, -1)], axis=1)
    etab = np.ascontiguousarray(etab, dtype=np.float32)
    vecs = [e_norm_pre[0], e_norm_post[0], mlp_norm_pre[0], mlp_norm_post[0], o_norm_pre[0], o_norm_post[0],
            mlp_norm_pre[1], mlp_norm_post[1], o_pool_scale[0]]
    gains = np.stack([f(v).reshape(8, 128).T for v in vecs], axis=1).reshape(128, 72)
    gains = np.ascontiguousarray(gains)
    sk = f(e_sink_b)[0]
    sinkrep = np.zeros((128, 256), np.float32)
    for hp in range(2):
        for j in range(4):
            sinkrep[64 * hp:64 * hp + 64, j * 64:(j + 1) * 64] = sk[2 * j + hp]
    ident = np.eye(128, dtype=np.float32)

    in_maps = []
    for c in range(NCORES):
        b, hf = c // 2, c % 2
        xc = np.zeros((XROWS, D), np.float32)
        pad = HALO_KV + HALO_Q
        if hf == 0:
            xc[pad:] = x[b, 0:OWN]
        else:
            xc[:] = x[b, OWN - pad:SEQ]
        vb_ = np.zeros((128, 6), np.float32)
        pm = np.ones((128, 16), np.float32)
        rc = np.zeros((128, 4, 16), np.float32)
        for gi in range(4):
            rc[:, gi, :] = 1.0 / (2 ** (gi + 1))
        if hf == 0:
            vb_[:, 0:5] = -30000.0
            pm[:] = 0.0
            for gi in range(4):
                w = 2 ** (gi + 1)
                rc[:, gi, :] = (1.0 / np.minimum(np.arange(16) + 1, w))[None, :]
        in_maps.append(dict(x=xc, wch=wch, etab=etab, gains=gains, vbias=vb_, sinkrep=sinkrep, pmask=pm,
                            rcnt=np.ascontiguousarray(rc.reshape(128, 64)), ident=ident))
    return in_maps


def kernel(**inputs):
    in_maps = prepare_inputs(**inputs)
    if 'nc' not in _NC_CACHE:
        _NC_CACHE['nc'] = build_program()
    res = run_bass_kernel_spmd(_NC_CACHE['nc'], in_maps, core_ids=list(range(NCORES)))
    outp = np.zeros((4, SEQ, D), np.float32)
    for c in range(NCORES):
        b, hf = c // 2, c % 2
        outp[b, hf * OWN:(hf + 1) * OWN] = res.results[c]["out"]
    return outp
```

```python
import math
from contextlib import ExitStack
import numpy as np
import concourse.bass as bass
import concourse.mybir as mybir
from concourse.bass_utils import run_bass_kernel_spmd

F32 = mybir.dt.float32
BF16 = mybir.dt.bfloat16
AF = mybir.ActivationFunctionType
ALU = mybir.AluOpType

NCORES = 8
D = 1024
KT = 8
SEQ = 8192
OWN = 4096
HALO_KV = 512
HALO_Q = 128
XROWS = OWN + HALO_KV + HALO_Q
MT_TILES = [6, 6, 6, 5, 5, 5]
TMAX = 128 * max(MT_TILES)
NSLOT = 4
EPS = 1e-6

ENGS = ['pe', 'act', 'dve', 'pool', 'sp']
SAME_SYNC = True


class Prog:
    def __init__(self, nc, es):
        self.nc = nc
        self.es = es
        self.ops = []
        self.lastw = {}
        self.readers = {}
        self.chan_cnt = {}
        self.final_chans = []
        self.fences = {}
        self.last_eng = {}
        self.last_chan = {}

    def fence(self, prefixes):
        d = set(self.last_eng.values()) | set(self.last_chan.values())
        for p in prefixes:
            self.fences[p] = d

    def op(self, eng, fn, r=(), w=(), dma=None):
        i = len(self.ops)
        deps = set()

        for x in r:
            if x in self.lastw:
                deps.add(self.lastw[x])
        for x in w:
            if x in self.lastw:
                deps.add(self.lastw[x])
            deps.update(self.readers.get(x, ()))
        for x in list(r) + list(w):
            p = x[0] if isinstance(x, tuple) else x
            if p in self.fences:
                deps.update(self.fences[p])
        for x in r:
            self.readers.setdefault(x, []).append(i)
        for x in w:
            self.lastw[x] = i
            self.readers[x] = []
        dval = None
        if dma is not None:
            self.chan_cnt[dma] = self.chan_cnt.get(dma, 0) + 1
            dval = self.chan_cnt[dma]
            self.last_chan[dma] = i
        else:
            self.last_eng[eng] = i
        self.ops.append(dict(eng=eng, fn=fn, deps=deps, dma=dma, dval=dval, idx=i, cval=None))
        return i

    def build(self):
        nc, es = self.nc, self.es
        ops = self.ops
        eng_ops = {e: [o for o in ops if o['eng'] == e] for e in ENGS}
        need = set()
        for o in ops:
            for d in o['deps']:
                do = ops[d]
                if do['dma'] is not None:
                    continue
                if do['eng'] != o['eng'] or (SAME_SYNC and o['eng'] != 'pe'):
                    need.add(d)
        for e, lst in eng_ops.items():
            c = 0
            for o in lst:
                if o['dma'] is None and o['idx'] in need:
                    c += 1
                    o['cval'] = c
        esem = {e: es.enter_context(nc.semaphore('s_' + e)) for e in ENGS}
        csem = {c: es.enter_context(nc.semaphore('c_%d' % k)) for k, c in enumerate(self.chan_cnt)}
        self.stats = {e: len(l) for e, l in eng_ops.items()}
        nwaits = {e: 0 for e in ENGS}

        def emit(E, eng):
            known = {}
            for o in eng_ops[E]:
                waits = {}
                for d in o['deps']:
                    do = ops[d]
                    if do['dma'] is not None:
                        key = ('c', do['dma'])
                        val = 16 * (self.chan_cnt[do['dma']] if do['dma'] == 'const' else do['dval'])
                    else:
                        if do['eng'] == E and (E == 'pe' or not SAME_SYNC):
                            continue
                        key = ('e', do['eng'])
                        val = do['cval']
                    if waits.get(key, 0) < val:
                        waits[key] = val
                for key, val in waits.items():
                    if known.get(key, 0) < val:
                        sem = csem[key[1]] if key[0] == 'c' else esem[key[1]]
                        eng.wait_ge(sem, val)
                        known[key] = val
                        nwaits[E] += 1
                ins = o['fn'](eng)
                if o['dma'] is not None:
                    ins.then_inc(csem[o['dma']], 16)
                elif o['cval'] is not None:
                    ins.then_inc(esem[E], 1)
            if E == 'sp':
                for c in self.final_chans:
                    eng.wait_ge(csem[c], 16 * self.chan_cnt[c])

        block = es.enter_context(nc.Block())

        @block.tensor
        def _(e):
            emit('pe', e)

        @block.scalar
        def _(e):
            emit('act', e)

        @block.vector
        def _(e):
            emit('dve', e)

        @block.gpsimd
        def _(e):
            emit('pool', e)

        @block.sync
        def _(e):
            emit('sp', e)
        self.nwaits = nwaits


C_QA, C_QB, C_KA, C_VA, C_KVB, C_WO0, C_WO1 = 0, 1, 2, 3, 4, 5, 6
C_UP0, C_DN0, C_POOL, C_UP1, C_DN1 = 7, 15, 23, 24, 32
NCHUNK = 40
G_EPRE, G_EPOST, G_M0PRE, G_M0POST, G_OPRE, G_OPOST, G_M1PRE, G_M1POST, G_PSCALE = range(9)
EA_W = 640
EB_W = 256


class _Stop(Exception):
    pass


def build_program(debug=False, stop=None):
    nc = bass.Bass("TRN2", target_bir_lowering=False)
    x = nc.dram_tensor("x", [XROWS, D], F32, kind="ExternalInput").ap()
    wch = nc.dram_tensor("wch", [NCHUNK, 128, 4096], F32, kind="ExternalInput").ap()
    etab_d = nc.dram_tensor("etab", [128, 8 * (EA_W + EB_W)], F32, kind="ExternalInput").ap()
    gains_d = nc.dram_tensor("gains", [128, 9 * 8], F32, kind="ExternalInput").ap()
    vbias_d = nc.dram_tensor("vbias", [128, 18], F32, kind="ExternalInput").ap()
    sink_d = nc.dram_tensor("sinkrep", [128, 256], F32, kind="ExternalInput").ap()
    pmask_d = nc.dram_tensor("pmask", [128, 16], F32, kind="ExternalInput").ap()
    rcnt_d = nc.dram_tensor("rcnt", [128, 64], F32, kind="ExternalInput").ap()
    ident_d = nc.dram_tensor("ident", [128, 128], F32, kind="ExternalInput").ap()
    out = nc.dram_tensor("out", [OWN, D], F32, kind="ExternalOutput").ap()

    es = ExitStack()
    with es:
        P = Prog(nc, es)

        def sb(name, shape, dt):
            return es.enter_context(nc.sbuf_tensor(name, shape, dt))

        hT = sb("hT", [128, KT, TMAX], F32)
        yT = sb("yT", [128, KT, TMAX], F32)
        actT = sb("actT", [128, KT, TMAX], BF16)
        KTb = sb("KTb", [128, 5, 512 + TMAX], BF16)
        Vb = sb("Vb", [128, 4 + TMAX // 128, 640], BF16)
        wb = [sb("wb%d" % i, [128, 4096], BF16) for i in range(NSLOT)]
        xs = [sb("xs%d" % i, [128, D], F32) for i in range(2)]
        sqm = sb("sqm", [128, KT, TMAX], BF16)
        rstd = sb("rstd", [128, 512], F32)
        tmpf = [sb("tmpf%d" % i, [128, TMAX // 2], F32) for i in range(2)]
        ident = sb("identS", [128, 128], F32)
        ones = sb("ones", [128, 128], BF16)
        gains = sb("gainsS", [128, 9, 8], F32)
        vbias = sb("vbiasS", [128, 18], F32)
        sinkE = sb("sinkE", [128, 256], F32)
        pmask = sb("pmaskS", [128, 16], F32)
        rcnt = sb("rcntS", [128, 4, 16], F32)
        epsc = sb("epsc", [128, 1], F32)
        carry = sb("carry", [128, KT, 16], F32)
        rD = [sb("rD%d" % i, [128, 256], F32) for i in range(2)]
        NR = max(4 * TMAX + 2 * 7 * 256 + 8 * (EA_W + EB_W) + 3 * 512, 16 * TMAX, 12 * (TMAX + 16))
        R = sb("R", [128, NR], F32)
        off = [0]

        def carve(nwords):
            a = off[0]
            off[0] += nwords
            assert off[0] <= NR, (off[0], NR)
            return R[:, a:a + nwords]

        uT = R[:, 0:16 * TMAX].bitcast(BF16).rearrange("p (f t) -> p f t", f=32)
        off[0] = 0
        QT = carve(4 * TMAX).bitcast(BF16).rearrange("p (k t) -> p k t", k=8)
        PT = [carve(7 * 256).bitcast(BF16).rearrange("p (k t) -> p k t", k=7) for _ in range(2)]
        Ea = carve(8 * EA_W).rearrange("p (h j) -> p h j", h=8)
        Eb = carve(8 * EB_W).rearrange("p (h j) -> p h j", h=8)
        sc = [carve(256) for _ in range(6)]
        off[0] = 0
        hn1 = carve(8 * (TMAX + 16)).rearrange("p (k t) -> p k t", k=8)
        sA = carve(2 * (TMAX + 16)).rearrange("p (k t) -> p k t", k=2)
        sB = carve(2 * (TMAX + 16)).rearrange("p (k t) -> p k t", k=2)

        ps = [es.enter_context(nc.psum_tensor("ps%d" % i, [128, 512], F32)) for i in range(8)]
        bankc = [0]

        def nextbank():
            b = bankc[0] % 8
            bankc[0] += 1
            return b

        evc = [0]
        scn = [0]

        def evac_eng():
            evc[0] += 1
            return 'act' if evc[0] % 2 else 'dve'

        def g(name, c0, n):
            return [(name, i) for i in range(c0 // 64, (c0 + n + 63) // 64)]

        def MM(o, lhsT, rhs, start, stop, r, w):
            P.op('pe', lambda e: e.matmul(o, lhsT=lhsT, rhs=rhs, start=start, stop=stop), r=r, w=w)

        def TR(o, in_, r, w):
            P.op('pe', lambda e: e.transpose(out=o, in_=in_, identity=ident[:]), r=list(r) + ['ident'], w=w)

        def ACT(o, in_, func, r, w, bias=None, scale=None):
            kw = {}
            if bias is not None:
                kw['bias'] = bias
            if scale is not None:
                kw['scale'] = scale
            P.op('act', lambda e: e.activation(out=o, in_=in_, func=func, **kw), r=r, w=w)

        def COPY(eng, o, in_, r, w):
            if eng == 'act':
                ACT(o, in_, AF.Copy, r, w)
            else:
                P.op(eng, lambda e: e.tensor_copy(out=o, in_=in_), r=r, w=w)

        def TT(eng, o, in0, in1, op, r, w):
            P.op(eng, lambda e: e.tensor_tensor(out=o, in0=in0, in1=in1, op=op), r=r, w=w)

        def STT(eng, o, in0, scalar, in1, op0, op1, r, w):
            P.op(eng, lambda e: e.scalar_tensor_tensor(out=o, in0=in0, scalar=scalar, in1=in1, op0=op0, op1=op1), r=r, w=w)

        def TS(eng, o, in0, s1, op0, r, w):
            P.op(eng, lambda e: e.tensor_scalar(out=o, in0=in0, scalar1=s1, scalar2=None, op0=op0), r=r, w=w)

        def RECIP(o, in_, r, w):
            P.op('dve', lambda e: e.reciprocal(out=o, in_=in_), r=r, w=w)

        def DMA(eng, o, in_, r, w, chan):
            P.op(eng, lambda e: e.dma_start(out=o, in_=in_), r=r, w=w, dma=chan)

        DMA('sp', ident[:], ident_d[:, :], [], ['ident'], 'const')
        if stop == 'c1':
            P.build()
            return nc
        DMA('sp', gains[:], gains_d.rearrange("p (g k) -> p g k", g=9), [], ['gains'], 'const')
        DMA('sp', vbias[:], vbias_d[:, :], [], ['vbias'], 'const')
        DMA('sp', sinkE[:], sink_d[:, :], [], ['sinkE'], 'const')
        DMA('sp', pmask[:], pmask_d[:, :], [], ['pmask'], 'const')
        DMA('sp', rcnt[:], rcnt_d.rearrange("p (g t) -> p g t", g=4), [], ['rcnt'], 'const')
        if stop == 'c2':
            P.build()
            return nc
        P.op('pool', lambda e: e.memset(ones[:], 1.0), w=['ones'])
        P.op('pool', lambda e: e.memset(epsc[:], EPS), w=['epsc'])
        P.op('pool', lambda e: e.memset(carry[:], 0.0), w=['carry'])
        if stop == 'c3':
            P.build()
            return nc
        ACT(sinkE[:], sinkE[:], AF.Exp, ['sinkE'], ['sinkE'])
        if stop == 'consts':
            P.build()
            return nc

        per_mt = ([C_QA, C_QB, C_KA, C_VA, C_KVB, C_WO0, C_WO1] + [C_UP0 + i for i in range(8)] +
                  [C_DN0 + i for i in range(8)] + [C_POOL] + [C_UP1 + i for i in range(8)] +
                  [C_DN1 + i for i in range(8)])
        stream = [C_KA, C_VA, C_KVB] + per_mt * len(MT_TILES)
        wst = dict(cur=-1, nxt=0)

        def acquire(expect):
            wst['cur'] += 1
            i = wst['cur']
            assert stream[i] == expect, (i, stream[i], expect)
            while wst['nxt'] < len(stream) and wst['nxt'] < i + NSLOT:
                j = wst['nxt']
                DMA('pool', wb[j % NSLOT][:], wch[stream[j]], [], [('w', j % NSLOT)], ('w', j % NSLOT))
                wst['nxt'] += 1
            return i % NSLOT

        xsc = [0]

        def load_x(row0, ntiles, col0):
            for i in range(ntiles):
                s = xsc[0] % 2
                xsc[0] += 1
                DMA('sp', xs[s][:], x[row0 + 128 * i: row0 + 128 * (i + 1), :], [], [('xs', s)], ('xs', s))
                for half in range(2):
                    b = nextbank()
                    for q in range(4):
                        kt = half * 4 + q
                        TR(ps[b][:, q * 128:(q + 1) * 128], xs[s][:, kt * 128:(kt + 1) * 128], [('xs', s)], [('ps', b)])
                    c = col0 + 128 * i
                    COPY(evac_eng(), hT[:, half * 4:half * 4 + 4, c:c + 128],
                         ps[b][:].rearrange("p (k t) -> p k t", k=4), [('ps', b)], g('hT', c, 128))

        def bc8(ap2d, n):
            return ap2d.unsqueeze(1).to_broadcast([128, KT, n])

        def rstd_from_sq(c0, n):
            b = nextbank()
            for kt in range(KT):
                MM(ps[b][:, 0:n], ones[:, :], sqm[:, kt, c0:c0 + n], kt == 0, kt == KT - 1,
                   g('sqm', c0, n) + ['ones'], [('ps', b)])
            ACT(rstd[:, 0:n], ps[b][:, 0:n], AF.Ln, [('ps', b), 'epsc'], ['rstd'], bias=epsc[:], scale=1.0 / D)
            ACT(rstd[:, 0:n], rstd[:, 0:n], AF.Exp, ['rstd'], ['rstd'], scale=-0.5)

        def norm_block(src, sname, c0, n, gi, mode, dst=None, dname=None, dc0=0):
            if mode == 'pre':
                ACT(sqm[:, :, c0:c0 + n], src[:, :, c0:c0 + n], AF.Square, g(sname, c0, n), g('sqm', c0, n))
                rstd_from_sq(c0, n)
                for kt in range(KT):
                    STT('dve', dst[:, kt, dc0:dc0 + n], src[:, kt, c0:c0 + n], gains[:, gi, kt:kt + 1], rstd[:, 0:n],
                        ALU.mult, ALU.mult, g(sname, c0, n) + ['rstd', 'gains'], g(dname, dc0, n))
            else:
                rstd_from_sq(c0, n)
                chk('m_rstd')
                TT('dve', yT[:, :, c0:c0 + n], yT[:, :, c0:c0 + n], bc8(rstd[:, 0:n], n), ALU.mult,
                   g('yT', c0, n) + ['rstd'], g('yT', c0, n))
                chk('m_bc')
                TT('dve', hT[:, :, c0:c0 + n], hT[:, :, c0:c0 + n], yT[:, :, c0:c0 + n], ALU.add,
                   g('yT', c0, n) + g('hT', c0, n), g('hT', c0, n))

        def evac_y(b, dt, c0, n, gi, pre_scale=None):
            if pre_scale is None:
                ACT(sqm[:, dt, c0:c0 + n], ps[b][:, 0:n], AF.Square, [('ps', b)], g('sqm', c0, n) + ['ser'])
                chk('m_sq')
                P.op('dve', lambda e: e.tensor_scalar(out=yT[:, dt, c0:c0 + n], in0=ps[b][:, 0:n],
                                                      scalar1=gains[:, gi, dt:dt + 1], scalar2=None, op0=ALU.mult),
                     r=[('ps', b), 'gains', 'ser'], w=g('yT', c0, n))
                chk('m_evac')
            else:
                ACT(sqm[:, dt, c0:c0 + n], ps[b][:, 0:n], AF.Square, [('ps', b), 'gains'], g('sqm', c0, n) + ['ser'], scale=pre_scale)
                P.op('dve', lambda e: e.tensor_scalar(out=yT[:, dt, c0:c0 + n], in0=ps[b][:, 0:n], scalar1=pre_scale,
                                                      scalar2=gains[:, gi, dt:dt + 1], op0=ALU.mult, op1=ALU.mult),
                     r=[('ps', b), 'gains', 'ser'], w=g('yT', c0, n))

        def proj_fm(slot, wc0, blocks, evac):
            for (c0, n) in blocks:
                b = nextbank()
                for kt in range(KT):
                    MM(ps[b][:, 0:n], wb[slot][:, kt * 512 + wc0: kt * 512 + wc0 + 128], actT[:, kt, c0:c0 + n],
                       kt == 0, kt == KT - 1, [('w', slot)] + g('actT', c0, n), [('ps', b)])
                evac(b, c0, n)

        def kv_proj(tiles0, ntile, blocks):
            kcol = 128 * tiles0
            s = acquire(C_KA)
            for ft in range(4):
                proj_fm(s, ft * 128, blocks,
                        lambda b, c0, n, ft=ft: COPY(evac_eng(), KTb[:, ft, kcol + c0:kcol + c0 + n], ps[b][:, 0:n],
                                                     [('ps', b)], g('KT', kcol + c0, n)))
            s = acquire(C_VA)
            for tt in range(ntile):
                b = nextbank()
                for kt in range(KT):
                    MM(ps[b][:, :], actT[:, kt, tt * 128:(tt + 1) * 128], wb[s][:, kt * 512:(kt + 1) * 512],
                       kt == 0, kt == KT - 1, [('w', s)] + g('actT', tt * 128, 128), [('ps', b)])
                COPY(evac_eng(), Vb[:, tiles0 + tt, 0:512], ps[b][:, :], [('ps', b)], [('V', tiles0 + tt)])
            s = acquire(C_KVB)
            proj_fm(s, 0, blocks,
                    lambda b, c0, n: COPY(evac_eng(), KTb[:, 4, kcol + c0:kcol + c0 + n], ps[b][:, 0:n],
                                          [('ps', b)], g('KT', kcol + c0, n)))
            for tt in range(ntile):
                b = nextbank()
                for kt in range(KT):
                    MM(ps[b][:, 0:128], actT[:, kt, tt * 128:(tt + 1) * 128], wb[s][:, kt * 512 + 128: kt * 512 + 256],
                       kt == 0, kt == KT - 1, [('w', s)] + g('actT', tt * 128, 128), [('ps', b)])
                COPY(evac_eng(), Vb[:, tiles0 + tt, 512:640], ps[b][:, 0:128], [('ps', b)], [('V', tiles0 + tt)])

        def key_tiles(lo, hi):
            res = []
            kc = (lo // 128) * 128
            while kc < hi:
                var = 1 if kc < lo else (2 if kc + 128 > hi else 0)
                res.append((kc, 0, 128, var))
                kc += 128
            return res

        SC_PAIRS = [(0, 1), (2, 3)]
        spair = [0]

        def attn_scores(ms, q0, pb):
            tl = [(False, ti, t) for ti, t in enumerate(key_tiles(q0, q0 + 576))] + \
                 [(True, 5 + ti, t) for ti, t in enumerate(key_tiles(q0 + 384, q0 + 576))]
            for k0 in range(0, len(tl), 2):
                bA, bB = SC_PAIRS[spair[0] % 2]
                spair[0] += 1
                grp = tl[k0:k0 + 2]
                for k, (is_b, pti, (kc0, p0, p1, var)) in enumerate(grp):
                    co = k * 256
                    kr = g('KT', kc0 + p0, p1 - p0)
                    if is_b:
                        for n2 in range(2):
                            bb = bA if n2 == 0 else bB
                            MM(ps[bb][p0:p1, co:co + 256], KTb[64 * n2:64 * n2 + 64, 4, kc0 + p0:kc0 + p1],
                               QT[64 * n2:64 * n2 + 64, 4:8, q0:q0 + 64], True, True, kr + g('QT', q0, 64), [('ps', bb)])
                    else:
                        for h in range(8):
                            hp, j = 64 * (h % 2), h // 2
                            bb = bA if h % 2 == 0 else bB
                            MM(ps[bb][p0:p1, co + j * 64:co + (j + 1) * 64], KTb[hp:hp + 64, j, kc0 + p0:kc0 + p1],
                               QT[hp:hp + 64, j, q0:q0 + 64], True, True, kr + g('QT', q0, 64), [('ps', bb)])
                for k, (is_b, pti, (kc0, p0, p1, var)) in enumerate(grp):
                    co = k * 256
                    E = Eb if is_b else Ea
                    joff = q0 + 512 - kc0
                    tok0 = ms + kc0 - 512
                    vcol = 6 * var + ((tok0 + 640) // 128 if tok0 < 0 else 5)
                    for par in range(2):
                        bb = bA if par == 0 else bB
                        scn[0] += 1
                        s = scn[0] % len(sc)
                        if is_b:
                            esl = E[p0:p1, 4 * par:4 * par + 4, joff:joff + 64]
                            pto = PT[pb][p0:p1, pti, par * 256:(par + 1) * 256]
                            sci = sc[s][p0:p1, :]
                        else:
                            esl = E[p0:p1, :, joff:joff + 64].rearrange("p (j two) q -> p j two q", two=2)[:, :, par, :]
                            pto = PT[pb][p0:p1, pti, :].rearrange("p (j two q) -> p j two q", two=2, q=64)[:, :, par, :]
                            sci = sc[s][p0:p1, :].rearrange("p (h q) -> p h q", h=4)
                        TT('dve', sc[s][p0:p1, :].rearrange("p (h q) -> p h q", h=4),
                           ps[bb][p0:p1, co:co + 256].rearrange("p (h q) -> p h q", h=4), esl, ALU.add,
                           [('ps', bb), 'E'], [('sc', s)])
                        ACT(pto, sci, AF.Exp, [('sc', s), 'vbias'], [('PT', pb, pti)], bias=vbias[p0:p1, vcol:vcol + 1])

        def attn_pv(q0, pb, is_b):
            tiles = key_tiles(q0 + 384, q0 + 576) if is_b else key_tiles(q0, q0 + 576)
            pt0 = 5 if is_b else 0
            bo = 4 + 2 * pb + (1 if is_b else 0)
            nt = len(tiles)
            for h in range(8):
                hp, j = 64 * (h % 2), h // 2
                vc0 = (512 + 64 * (h // 4)) if is_b else h * 64
                for ti, (kc0, p0, p1, var) in enumerate(tiles):
                    vt = kc0 // 128
                    MM(ps[bo][hp:hp + 64, j * 64:(j + 1) * 64], Vb[p0:p1, vt, vc0:vc0 + 64],
                       PT[pb][p0:p1, pt0 + ti, h * 64:(h + 1) * 64], ti == 0, ti == nt - 1,
                       [('V', vt), ('PT', pb, pt0 + ti)], [('ps', bo)])
            for hpar in range(2):
                for ti, (kc0, p0, p1, var) in enumerate(tiles):
                    rhs = PT[pb][p0:p1, pt0 + ti, :].rearrange("p (j two q) -> p j two q", two=2, q=64)[:, :, hpar, :]
                    MM(ps[bo][64 * hpar:64 * hpar + 64, 256:512], ones[p0:p1, 0:64], rhs, ti == 0, ti == nt - 1,
                       [('PT', pb, pt0 + ti), 'ones'], [('ps', bo)])
            rd = rD[1 if is_b else 0]
            rn = ('rD', 1 if is_b else 0)
            if is_b:
                TT('dve', rd[:, :], ps[bo][:, 256:512], sinkE[:, :], ALU.add, [('ps', bo), 'sinkE'], [rn])
                RECIP(rd[:, :], rd[:, :], [rn], [rn])
            else:
                RECIP(rd[:, :], ps[bo][:, 256:512], [('ps', bo)], [rn])
            a0 = 4 if is_b else 0
            TT('dve', actT[:, a0:a0 + 4, q0:q0 + 64], ps[bo][:, 0:256].rearrange("p (j q) -> p j q", j=4),
               rd[:, :].rearrange("p (j q) -> p j q", j=4), ALU.mult, [('ps', bo), rn], g('actT', q0, 64))

        def mlp(layer, T, blocks, gpre, gpost, c_up, c_dn):
            P.fence(['uT', 'rl'])
            for (c0, n) in blocks:
                norm_block(hT, 'hT', c0, n, gpre, 'pre', actT, 'actT', c0)
            for fc in range(8):
                s = acquire(c_up + fc)
                for fq in range(4):
                    def ev(b, c0, n, f=fc * 4 + fq):
                        k = f % 2
                        ACT(tmpf[k][:, 0:n], ps[b][:, 0:n], AF.Relu, [('ps', b)], [('tmpf', k)])
                        TT('dve', uT[:, f, c0:c0 + n], tmpf[k][:, 0:n], ps[b][:, 0:n], ALU.mult,
                           [('tmpf', k), ('ps', b)], g('uT', c0, n))
                    proj_fm(s, fq * 128, blocks, ev)
            for dt in range(8):
                s = acquire(c_dn + dt)
                for (c0, n) in blocks:
                    b = nextbank()
                    for ft in range(32):
                        MM(ps[b][:, 0:n], wb[s][:, ft * 128:(ft + 1) * 128], uT[:, ft, c0:c0 + n], ft == 0, ft == 31,
                           [('w', s)] + g('uT', c0, n), [('ps', b)])
                    evac_y(b, dt, c0, n, gpost)
            for (c0, n) in blocks:
                norm_block(yT, 'yT', c0, n, gpost, 'post')

        def chk(tag):
            if stop is not None and tag == stop:
                raise _Stop()

        try:
            if stop == 'x1':
                DMA('sp', xs[0][:], x[0:128, :], [], [('xs', 0)], ('xs', 0))
                raise _Stop()
            if stop == 'x2':
                DMA('sp', xs[0][:], x[0:128, :], [], [('xs', 0)], ('xs', 0))
                TR(ps[0][:, 0:128], xs[0][:, 0:128], [('xs', 0)], [('ps', 0)])
                raise _Stop()
            if stop == 'x3':
                DMA('sp', xs[0][:], x[0:128, :], [], [('xs', 0)], ('xs', 0))
                TR(ps[0][:, 0:128], xs[0][:, 0:128], [('xs', 0)], [('ps', 0)])
                COPY('dve', hT[:, 0, 0:128], ps[0][:, 0:128], [('ps', 0)], g('hT', 0, 128))
                raise _Stop()
            if stop in ('t2', 't3'):
                load_x(0, int(stop[1]), 0)
                raise _Stop()
            if stop == 'x4':
                load_x(0, 1, 0)
                raise _Stop()
            load_x(0, 4, 0)
            chk('pre_load')
            norm_block(hT, 'hT', 0, 512, G_EPRE, 'pre', actT, 'actT', 0)
            chk('pre_norm')
            kv_proj(0, 4, [(0, 512)])
            chk('pre')

            ms = -HALO_Q
            orow = 0
            for mi, ntile in enumerate(MT_TILES):
                T = 128 * ntile
                half = T // 2
                blocks = [(0, half), (half, T - half)]
                load_x(ms + HALO_KV + HALO_Q, ntile, 0)
                P.fence(['QT', 'PT', 'E', 'sc'])
                DMA('sp', R[:, 4 * TMAX + 2 * 7 * 256: 4 * TMAX + 2 * 7 * 256 + 8 * (EA_W + EB_W)], etab_d[:, :], [], ['E'], 'etab')
                for (c0, n) in blocks:
                    norm_block(hT, 'hT', c0, n, G_EPRE, 'pre', actT, 'actT', c0)
                for ci, cid in enumerate((C_QA, C_QB)):
                    s = acquire(cid)
                    for ft in range(4):
                        proj_fm(s, ft * 128, blocks,
                                lambda b, c0, n, ft=ft, ci=ci: ACT(QT[:, 4 * ci + ft, c0:c0 + n], ps[b][:, 0:n], AF.Copy,
                                                                   [('ps', b)], g('QT', c0, n), scale=0.125))
                kv_proj(4, ntile, blocks)
                chk('qkv')
                nck = T // 64
                for ck in range(nck + 1):
                    if ck < nck:
                        attn_scores(ms, 64 * ck, ck % 2)
                    if ck >= 1:
                        attn_pv(64 * (ck - 1), (ck - 1) % 2, False)
                        attn_pv(64 * (ck - 1), (ck - 1) % 2, True)
                chk('attn')
                for dh in range(2):
                    s = acquire(C_WO0 + dh)
                    for dq in range(4):
                        proj_fm(s, dq * 128, blocks,
                                lambda b, c0, n, dt=dh * 4 + dq: evac_y(b, dt, c0, n, G_EPOST))
                for (c0, n) in blocks:
                    norm_block(yT, 'yT', c0, n, G_EPOST, 'post')
                if mi + 1 < len(MT_TILES):
                    COPY('pool', KTb[:, :, 0:512], KTb[:, :, T:T + 512], g('KT', T, 512), g('KT', 0, 512))
                    COPY('pool', Vb[:, 0:4, :], Vb[:, ntile:ntile + 4, :], [('V', ntile + i) for i in range(4)],
                         [('V', i) for i in range(4)])
                chk('mixer')
                mlp(0, T, blocks, G_M0PRE, G_M0POST, C_UP0, C_DN0)
                chk('mlp0')
                P.fence(['hn1', 'sA', 'sB'])
                COPY('pool', hn1[:, :, 0:16], carry[:, :, :], ['carry'], [('hn1', 0)])
                for (c0, n) in blocks:
                    norm_block(hT, 'hT', c0, n, G_OPRE, 'pre', hn1, 'hn1', 16 + c0)
                if mi + 1 < len(MT_TILES):
                    COPY('pool', carry[:, :, :], hn1[:, :, T:T + 16], g('hn1', T, 16), ['carry'])
                if mi == 0:
                    for kt in range(KT):
                        TT('pool', hn1[:, kt, 16 + 112:16 + 128], hn1[:, kt, 16 + 112:16 + 128], pmask[:, :], ALU.mult,
                           g('hn1', 128, 16) + ['pmask'], g('hn1', 128, 16))
                W = T + 16
                hr = g('hn1', 0, W)
                for gi in range(4):
                    k0 = 2 * gi
                    TT('dve', sA[:, :, 1:W], hn1[:, k0:k0 + 2, 1:W], hn1[:, k0:k0 + 2, 0:W - 1], ALU.add, hr, ['sA'])
                    cur, oth, sh = sA, sB, 2
                    cn, on = 'sA', 'sB'
                    for lvl in range(gi):
                        lo = 2 * sh - 1
                        TT('dve', oth[:, :, lo:W], cur[:, :, lo:W], cur[:, :, lo - sh:W - sh], ALU.add, [cn], [on])
                        cur, oth, cn, on = oth, cur, on, cn
                        sh *= 2
                    wsz = 2 ** (gi + 1)
                    STT('dve', actT[:, k0:k0 + 2, 0:T], cur[:, :, 16:W], 1.0 / wsz, hn1[:, k0:k0 + 2, 16:W],
                        ALU.mult, ALU.subtract, [cn] + hr, g('actT', 0, T))
                    if mi == 0:
                        for k in range(2):
                            TT('pool', tmpf[0][:, 0:16], cur[:, k, 16 + 128:16 + 144], rcnt[:, gi, :], ALU.mult,
                               [cn, 'rcnt'], [('tmpf', 0)])
                            TT('pool', actT[:, k0 + k, 128:144], tmpf[0][:, 0:16], hn1[:, k0 + k, 16 + 128:16 + 144],
                               ALU.subtract, [('tmpf', 0)] + hr, g('actT', 128, 16))
                s = acquire(C_POOL)
                for gi in range(4):
                    for et in range(2):
                        for (c0, n) in blocks:
                            b = nextbank()
                            for ct in range(2):
                                wc = gi * 512 + ct * 256 + et * 128
                                MM(ps[b][:, 0:n], wb[s][:, wc:wc + 128], actT[:, 2 * gi + ct, c0:c0 + n], ct == 0, ct == 1,
                                   [('w', s)] + g('actT', c0, n), [('ps', b)])
                            evac_y(b, 2 * gi + et, c0, n, G_OPOST,
                                   pre_scale=gains[:, G_PSCALE, 2 * gi + et:2 * gi + et + 1])
                for (c0, n) in blocks:
                    norm_block(yT, 'yT', c0, n, G_OPOST, 'post')
                chk('pool')
                mlp(1, T, blocks, G_M1PRE, G_M1POST, C_UP1, C_DN1)
                chk('mlp1')
                for tt in range(ntile):
                    if ms + 128 * tt < 0:
                        continue
                    s = xsc[0] % 2
                    xsc[0] += 1
                    for hf in range(2):
                        b = nextbank()
                        for q in range(4):
                            kt = hf * 4 + q
                            TR(ps[b][:, q * 128:(q + 1) * 128], hT[:, kt, tt * 128:(tt + 1) * 128], g('hT', tt * 128, 128), [('ps', b)])
                        COPY(evac_eng(), xs[s][:, hf * 512:(hf + 1) * 512], ps[b][:, :], [('ps', b)], [('xs', s)])
                    DMA('sp', out[orow:orow + 128, :], xs[s][:], [('xs', s)], [], ('xs', s))
                    orow += 128
                ms += T
            assert orow == OWN and wst['cur'] == len(stream) - 1
        except _Stop:
            pass
        P.final_chans = [c for c in [('xs', 0), ('xs', 1)] if c in P.chan_cnt]
        P.build()
        if debug:
            print("ops", P.stats, "waits", P.nwaits)
    return nc


def t5_bucket(rel_kq):
    nb = 16
    ret = (rel_kq > 0).astype(np.int32) * nb
    n = np.abs(rel_kq)
    max_exact = nb // 2
    large = max_exact + (np.log(np.maximum(n, 1) / max_exact) / math.log(128 / max_exact) * (nb - max_exact)).astype(np.int32)
    large = np.minimum(large, nb - 1)
    return ret + np.where(n < max_exact, n, large)


def pack_weights(e_w_in, e_w_out, o_pool_w, mlp_w_up, mlp_w_down):
    ch = np.zeros((NCHUNK, 128, 4096), np.float32)

    def km(w):
        C = w.shape[1]
        return w.reshape(8, 128, C).transpose(1, 0, 2).reshape(128, 8 * C)

    wi = e_w_in[0]
    qa, ka, va = wi[:, 0:512], wi[:, 512:1024], wi[:, 1024:1536]
    qb, kb, vb = wi[:, 1536:2048], wi[:, 2048:2176], wi[:, 2176:2304]
    qbp = np.concatenate([np.concatenate([qb[:, j * 64:(j + 1) * 64], qb[:, (4 + j) * 64:(5 + j) * 64]], axis=1)
                          for j in range(4)], axis=1)
    ch[C_QA] = km(qa)
    ch[C_QB] = km(qbp)
    ch[C_KA] = km(ka)
    ch[C_VA] = km(va)
    kvb = np.zeros((1024, 512), np.float32)
    kvb[:, 0:128] = kb
    kvb[:, 128:256] = vb
    ch[C_KVB] = km(kvb)
    wo = e_w_out[0]
    for dh in range(2):
        ch[C_WO0 + dh] = km(wo[:, dh * 512:(dh + 1) * 512])
    for l in range(2):
        cu = C_UP0 if l == 0 else C_UP1
        cd = C_DN0 if l == 0 else C_DN1
        for fc in range(8):
            ch[cu + fc] = km(mlp_w_up[l][:, fc * 512:(fc + 1) * 512])
        for dt in range(8):
            wd = mlp_w_down[l][:, dt * 128:(dt + 1) * 128]
            ch[cd + dt] = wd.reshape(32, 128, 128).transpose(1, 0, 2).reshape(128, 4096)
    pw = o_pool_w[0]
    pp = pw.reshape(4, 2, 128, 256).transpose(2, 0, 1, 3).reshape(128, 2048)
    ch[C_POOL][:, 0:2048] = pp
    return ch


_NC_CACHE = {}


def prepare_inputs(x, t5_table, e_norm_pre, e_norm_post, e_w_in, e_w_out, e_relpos_a, e_sink_b,
           o_norm_pre, o_norm_post, o_pool_w, o_pool_scale,
           mlp_norm_pre, mlp_norm_post, mlp_w_up, mlp_w_down):
    f = lambda a: np.ascontiguousarray(np.asarray(a, dtype=np.float32))
    x = f(x)
    wch = pack_weights(f(e_w_in), f(e_w_out), f(o_pool_w), f(mlp_w_up), f(mlp_w_down))
    p = np.arange(128)[:, None]
    ja = np.arange(EA_W)[None, :]
    idx_a = np.clip(ja - p, -128, 128) + 128
    Ea = f(e_relpos_a)[0][idx_a]
    jb = np.arange(EB_W)[None, :]
    idx_b = t5_bucket(-(jb - p))
    Eb = f(t5_table)[idx_b]
    etab = np.concatenate([Ea.transpose(0, 2, 1).reshape(128, -1), Eb.transpose(0, 2, 1).reshape(128, -1)], axis=1)
    etab = np.ascontiguousarray(etab, dtype=np.float32)
    vecs = [e_norm_pre[0], e_norm_post[0], mlp_norm_pre[0], mlp_norm_post[0], o_norm_pre[0], o_norm_post[0],
            mlp_norm_pre[1], mlp_norm_post[1], o_pool_scale[0]]
    gains = np.stack([f(v).reshape(8, 128).T for v in vecs], axis=1).reshape(128, 72)
    gains = np.ascontiguousarray(gains)
    sk = f(e_sink_b)[0]
    sinkrep = np.zeros((128, 256), np.float32)
    for hp in range(2):
        for j in range(4):
            sinkrep[64 * hp:64 * hp + 64, j * 64:(j + 1) * 64] = sk[2 * j + hp]
    ident = np.eye(128, dtype=np.float32)

    in_maps = []
    for c in range(NCORES):
        b, hf = c // 2, c % 2
        xc = np.zeros((XROWS, D), np.float32)
        pad = HALO_KV + HALO_Q
        if hf == 0:
            xc[pad:] = x[b, 0:OWN]
        else:
            xc[:] = x[b, OWN - pad:SEQ]
        vb_ = np.zeros((128, 18), np.float32)
        pm = np.ones((128, 16), np.float32)
        rc = np.zeros((128, 4, 16), np.float32)
        for gi in range(4):
            rc[:, gi, :] = 1.0 / (2 ** (gi + 1))
        if hf == 0:
            vb_[:, 0:5] = -60.0
            vb_[:, 6:11] = -60.0
            vb_[:, 12:17] = -60.0
            pm[:] = 0.0
            for gi in range(4):
                w = 2 ** (gi + 1)
                rc[:, gi, :] = (1.0 / np.minimum(np.arange(16) + 1, w))[None, :]
        vb_[0:64, 6:12] = -60.0
        vb_[64:128, 12:18] = -60.0
        in_maps.append(dict(x=xc, wch=wch, etab=etab, gains=gains, vbias=vb_, sinkrep=sinkrep, pmask=pm,
                            rcnt=np.ascontiguousarray(rc.reshape(128, 64)), ident=ident))
    return in_maps


def kernel(**inputs):
    in_maps = prepare_inputs(**inputs)
    if 'nc' not in _NC_CACHE:
        _NC_CACHE['nc'] = build_program()
    res = run_bass_kernel_spmd(_NC_CACHE['nc'], in_maps, core_ids=list(range(NCORES)))
    outp = np.zeros((4, SEQ, D), np.float32)
    for c in range(NCORES):
        b, hf = c // 2, c % 2
        outp[b, hf * OWN:(hf + 1) * OWN] = res.results[c]["out"]
    return outp
```
